# Optimizing a Trainium2 kernel written in Bass

```python
import jax, jax.numpy as jnp
from jax import lax
import numpy as np

D_MODEL = 1024
BATCH = 8
SEQ = 4096
DEPTH = 2

EPS = 1e-6
CONV_WIDTH = 3
A_HEADS = 8
A_HEAD_DIM = 64
A_WIDTH = A_HEADS * A_HEAD_DIM
POOL_WINDOWS = (2, 4, 8, 16)
B_GROUPS = len(POOL_WINDOWS)
B_GROUP_DIM = 128
B_WIDTH = B_GROUPS * B_GROUP_DIM
EVEN_IN = 3 * A_WIDTH + B_WIDTH
EVEN_MIX = A_WIDTH + B_WIDTH
CHUNK = 128
C_HEADS = 8
C_HEAD_DIM = 128
C_WIDTH = C_HEADS * C_HEAD_DIM
D_FF = 2816

N_EVEN = (DEPTH + 1) // 2
N_ODD = DEPTH // 2

kernel_name = "hybrid_conv_pool_sgu_trunk"


def rms_norm(x, g):
    x32 = x.astype(jnp.float32)
    y = x32 * lax.rsqrt(jnp.mean(x32 * x32, axis=-1, keepdims=True) + EPS)
    return y.astype(x.dtype) * g


def causal_dwconv3(x, w):
    s = x.shape[1]
    xp = jnp.pad(x, ((0, 0), (CONV_WIDTH - 1, 0), (0, 0)))
    y = xp[:, 0:s] * w[0]
    for k in range(1, CONV_WIDTH):
        y = y + xp[:, k:k + s] * w[k]
    return y


def short_gated_conv(gate_b, gate_c, val, conv_w):
    return gate_b * causal_dwconv3(gate_c * val, conv_w)


def multiscale_pool(z, w_pool, pool_scale):
    b, s, _ = z.shape
    z32 = z.astype(jnp.float32).reshape(b, s, B_GROUPS, B_GROUP_DIM)
    cs = jnp.cumsum(z32, axis=1)
    pos = jnp.arange(1, s + 1, dtype=jnp.float32)[None, :, None]
    outs = []
    for g, w in enumerate(POOL_WINDOWS):
        csg = cs[:, :, g]
        shifted = jnp.pad(csg, ((0, 0), (w, 0), (0, 0)))[:, :s]
        count = jnp.minimum(pos, jnp.float32(w))
        outs.append((csg - shifted) / count - z32[:, :, g])
    pooled = jnp.stack(outs, axis=2).astype(z.dtype)
    mixed = jnp.einsum('bsgc,gcd->bsgd', pooled, w_pool)
    return mixed.reshape(b, s, B_WIDTH) * pool_scale


def chunked_spatial_gating(u, v, sgu_norm, w_spatial, b_spatial):
    b, s, _ = u.shape
    n = s // CHUNK
    vn = rms_norm(v, sgu_norm).reshape(b, n, CHUNK, C_HEADS, C_HEAD_DIM)
    mask = jnp.tril(jnp.ones((CHUNK, CHUNK), dtype=w_spatial.dtype))
    ws = w_spatial * mask
    gate = jnp.einsum('hts,bnshc->bnthc', ws, vn) + b_spatial.T[None, None, :, :, None]
    return u * gate.reshape(b, s, C_WIDTH)


def gated_conv_ffn(x, w_gate, w_up, conv_w, conv_b, w_down):
    g = jnp.einsum('bsd,df->bsf', x, w_gate)
    g = causal_dwconv3(g, conv_w) + conv_b
    up = jnp.einsum('bsd,df->bsf', x, w_up)
    return jnp.einsum('bsf,fd->bsd', jax.nn.silu(g) * up, w_down)


def setup_inputs(seed: int = 0) -> dict:
    key = jax.random.key(seed)
    ks = jax.random.split(key, 20)
    f32 = jnp.float32
    nrm = lambda k, shape, s: jax.random.normal(k, shape, f32) * s
    return {
        "x": nrm(ks[0], (BATCH, SEQ, D_MODEL), 1.0),
        "norm_mix": 1.0 + nrm(ks[1], (DEPTH, D_MODEL), 0.05),
        "norm_ffn": 1.0 + nrm(ks[2], (DEPTH, D_MODEL), 0.05),
        "final_norm": 1.0 + nrm(ks[3], (D_MODEL,), 0.05),
        "w_in_even": nrm(ks[4], (N_EVEN, D_MODEL, EVEN_IN), D_MODEL ** -0.5),
        "conv_a": nrm(ks[5], (N_EVEN, CONV_WIDTH, A_WIDTH), CONV_WIDTH ** -0.5),
        "w_pool": nrm(ks[6], (N_EVEN, B_GROUPS, B_GROUP_DIM, B_GROUP_DIM), B_GROUP_DIM ** -0.5),
        "pool_scale": 1.0 + nrm(ks[7], (N_EVEN, B_WIDTH), 0.1),
        "w_out_even": nrm(ks[8], (N_EVEN, EVEN_MIX, D_MODEL), EVEN_MIX ** -0.5),
        "w_in_odd": nrm(ks[9], (N_ODD, D_MODEL, 2 * C_WIDTH), D_MODEL ** -0.5),
        "sgu_norm": 1.0 + nrm(ks[10], (N_ODD, C_WIDTH), 0.05),
        "w_spatial": nrm(ks[11], (N_ODD, C_HEADS, CHUNK, CHUNK), CHUNK ** -0.5),
        "b_spatial": 1.0 + nrm(ks[12], (N_ODD, C_HEADS, CHUNK), 0.05),
        "w_out_odd": nrm(ks[13], (N_ODD, C_WIDTH, D_MODEL), C_WIDTH ** -0.5),
        "w_ffn_gate": nrm(ks[14], (DEPTH, D_MODEL, D_FF), D_MODEL ** -0.5),
        "w_ffn_up": nrm(ks[15], (DEPTH, D_MODEL, D_FF), D_MODEL ** -0.5),
        "conv_ffn": nrm(ks[16], (DEPTH, CONV_WIDTH, D_FF), CONV_WIDTH ** -0.5),
        "b_conv_ffn": nrm(ks[17], (DEPTH, D_FF), 0.02),
        "w_ffn_down": nrm(ks[18], (DEPTH, D_FF, D_MODEL), D_FF ** -0.5),
    }


def reference(x, norm_mix, norm_ffn, final_norm, w_in_even, conv_a, w_pool, pool_scale,
              w_out_even, w_in_odd, sgu_norm, w_spatial, b_spatial, w_out_odd,
              w_ffn_gate, w_ffn_up, conv_ffn, b_conv_ffn, w_ffn_down):
    h = x
    for layer in range(DEPTH):
        xn = rms_norm(h, norm_mix[layer])
        if layer % 2 == 0:
            i = layer // 2
            proj = jnp.einsum('bsd,de->bse', xn, w_in_even[i])
            a_b = proj[..., 0:A_WIDTH]
            a_c = proj[..., A_WIDTH:2 * A_WIDTH]
            a_v = proj[..., 2 * A_WIDTH:3 * A_WIDTH]
            z_b = proj[..., 3 * A_WIDTH:]
            y_a = short_gated_conv(a_b, a_c, a_v, conv_a[i])
            y_b = multiscale_pool(z_b, w_pool[i], pool_scale[i])
            mix = jnp.concatenate([y_a, y_b], axis=-1)
            h = h + jnp.einsum('bse,ed->bsd', mix, w_out_even[i])
        else:
            i = layer // 2
            proj = jax.nn.gelu(jnp.einsum('bsd,de->bse', xn, w_in_odd[i]), approximate=False)
            u = proj[..., :C_WIDTH]
            v = proj[..., C_WIDTH:]
            mix = chunked_spatial_gating(u, v, sgu_norm[i], w_spatial[i], b_spatial[i])
            h = h + jnp.einsum('bse,ed->bsd', mix, w_out_odd[i])
        hn = rms_norm(h, norm_ffn[layer])
        h = h + gated_conv_ffn(hn, w_ffn_gate[layer], w_ffn_up[layer], conv_ffn[layer],
                               b_conv_ffn[layer], w_ffn_down[layer])
    return rms_norm(h, final_norm)
```

```python
import contextlib
import numpy as np
import concourse.bass as bass
import concourse.mybir as mybir
from concourse.bass_utils import run_bass_kernel_spmd

F32 = mybir.dt.float32
BF16 = mybir.dt.bfloat16
AF = mybir.ActivationFunctionType
ALU = mybir.AluOpType

D = 1024
S = 4096
TT = 512
DFF = 2816
NF = DFF // 128
EPS = 1e-6
NRING = 6
FUSE_WAIT = False
DIRECT_FIRST = True
RINGW = 4096

PAR = {}
_off = 0
for _name, _n in [("norm_mix0", 8), ("norm_mix1", 8), ("norm_ffn0", 8), ("norm_ffn1", 8),
                  ("final_norm", 8), ("conv_a", 12), ("pool_scale", 4),
                  ("conv_ffn0", 66), ("conv_ffn1", 66), ("b_ffn0", 22), ("b_ffn1", 22),
                  ("rc", 64), ("eps", 1)]:
    PAR[_name] = _off
    _off += _n
NPAR = _off

WSPEC = {
    "ine": (4, 4096), "oute": (2, 4096), "gu0": (11, 4096), "dn0": (8, 2816),
    "ino": (4, 4096), "outo": (2, 4096), "gu1": (11, 4096), "dn1": (8, 2816),
}


class Op:
    __slots__ = ("idx", "eng", "fn", "deps", "semname", "inc", "dur", "occ", "done", "value", "users", "prio", "table", "urgent")

    def __init__(self, idx, eng, fn, semname, inc, dur, occ):
        self.idx, self.eng, self.fn, self.semname, self.inc = idx, eng, fn, semname, inc
        self.dur, self.occ = dur, occ
        self.deps = set()
        self.done = idx
        self.prio = idx
        self.table = None
        self.urgent = False
        self.value = None
        self.users = []


class Sched:
    SEM_LAT = 0.12

    def __init__(self):
        self.ops = []
        self.engines = {}
        self.lw = {}
        self.rd = {}
        self.open_group = {}

    def add_engine(self, name, self_sync, window):
        self.engines[name] = dict(self_sync=self_sync, window=window, lookahead=0)
        self.open_group[name] = []

    DEFAULTS = {"pe": (0.40, 0.215), "act": (0.70, 0.70), "dve": (0.70, 0.70), "pool": (1.3, 1.3), "sp": (6.0, 0.06)}

    def op(self, eng, fn, reads=(), writes=(), sem=None, inc=1, dur=None, occ=None, lag=0, manual=False, table=None, urgent=False):
        semname = eng if sem is None else sem
        dd, do = self.DEFAULTS[eng]
        if dur is None:
            dur = dd
            if occ is None:
                occ = do
        elif occ is None:
            occ = dur
        o = Op(len(self.ops), eng, fn, semname, inc, dur, occ)
        o.prio = o.idx + lag
        o.table = table
        o.urgent = urgent
        self.ops.append(o)
        for k in reads:
            w = self.lw.get(k)
            if w is not None:
                o.deps.add(w)
        for k in writes:
            w = self.lw.get(k)
            if w is not None:
                o.deps.add(w)
            o.deps.update(self.rd.get(k, ()))
        o.deps.discard(o.idx)
        for k in reads:
            self.rd.setdefault(k, []).append(o.idx)
        for k in writes:
            self.lw[k] = o.idx
            self.rd[k] = []
        if manual:
            pass
        elif inc == 0:
            self.open_group[eng].append(o.idx)
        else:
            for i in self.open_group[eng]:
                self.ops[i].done = o.idx
            self.open_group[eng] = []
        return o.idx

    def group_done(self, idxs):
        last = max(idxs)
        for i in idxs:
            self.ops[i].done = last

    def schedule(self):
        ops = self.ops
        for o in ops:
            extra = set()
            for d in o.deps:
                dn = ops[d].done
                if dn != d and dn != o.idx:
                    assert dn < o.idx
                    extra.add(dn)
            o.deps |= extra
        for o in ops:
            for d in o.deps:
                ops[d].users.append(o.idx)
        queues = {e: [] for e in self.engines}
        for o in ops:
            queues[o.eng].append(o.idx)
        pos = {e: 0 for e in self.engines}
        sched = [False] * len(ops)
        finish = [0.0] * len(ops)
        free = {e: 0.0 for e in self.engines}
        order = {e: [] for e in self.engines}
        gorder = []
        cur_table = {e: None for e in self.engines}
        TBL = 1.3
        cand = {e: None for e in self.engines}
        dirty = {e: True for e in self.engines}
        remaining = len(ops)

        def best(e):
            q = queues[e]
            p = pos[e]
            while p < len(q) and sched[q[p]]:
                p += 1
            pos[e] = p
            W = self.engines[e]["window"]
            bst = None
            seen = 0
            i = p
            while i < len(q) and seen < W:
                oi = q[i]
                i += 1
                if sched[oi]:
                    continue
                seen += 1
                o = ops[oi]
                ok = True
                st = free[e]
                for d in o.deps:
                    if not sched[d]:
                        ok = False
                        break
                    t = finish[d] + self.SEM_LAT
                    if t > st:
                        st = t
                if not ok:
                    continue
                real = st
                if o.table is not None and o.table != cur_table[e]:
                    real = st + TBL
                    if not o.urgent:
                        st = real
                if bst is None or st < bst[0] - 1e-9 or (st < bst[0] + 1e-9 and o.prio < ops[bst[1]].prio):
                    bst = (st, oi, real)
            return bst

        while remaining:
            for e in self.engines:
                if dirty[e]:
                    cand[e] = best(e)
                    dirty[e] = False
            pick = None
            for e in self.engines:
                c = cand[e]
                if c is not None and (pick is None or c[0] < pick[0]):
                    pick = (c[0], c[1], e)
            assert pick is not None, "scheduler deadlock"
            st, oi, e = pick
            st = cand[e][2]
            o = ops[oi]
            if o.table is not None:
                cur_table[e] = o.table
            sched[oi] = True
            finish[oi] = st + o.dur
            free[e] = st + o.occ
            order[e].append(oi)
            gorder.append(oi)
            remaining -= 1
            dirty[e] = True
            for u in o.users:
                dirty[ops[u].eng] = True
        self.order = order
        self.gorder = gorder
        self.finish = finish
        self.start = {}
        for oi in gorder:
            self.start[oi] = finish[oi] - ops[oi].dur
        self.est_total = max(finish) if finish else 0.0

    def build_streams(self):
        ops = self.ops
        count = {}
        posn = {}
        for e, lst in self.order.items():
            for i, oi in enumerate(lst):
                posn[oi] = i
        for oi in self.gorder:
            o = ops[oi]
            if o.inc > 0:
                count[o.semname] = count.get(o.semname, 0) + o.inc
                o.value = count[o.semname]
        self.count = count
        streams = {}
        producer = {}
        for o in ops:
            if o.inc > 0:
                producer[(o.semname, o.value)] = o.idx
        for e, lst in self.order.items():
            st = []
            waited = {}
            ss = self.engines[e]["self_sync"]
            needs = []
            for oi in lst:
                o = ops[oi]
                need = {}
                for d in o.deps:
                    dn = ops[ops[d].done]
                    assert dn.value is not None
                    if dn.eng == e and dn.semname == e:
                        assert posn[d] < posn[oi]
                        if not ss:
                            continue
                        assert posn[dn.idx] < posn[oi], "same-engine wait on a later op"
                    if need.get(dn.semname, 0) < dn.value:
                        need[dn.semname] = dn.value
                needs.append(need)
            look = self.engines[e].get("lookahead", 0)
            for i, oi in enumerate(lst):
                o = ops[oi]
                need = {s: v for s, v in needs[i].items() if waited.get(s, 0) < v}
                if need and look:
                    t_now = self.start[oi]
                    for j in range(i + 1, min(len(lst), i + 1 + look)):
                        for s, v in needs[j].items():
                            if waited.get(s, 0) >= v or need.get(s, 0) >= v:
                                continue
                            pi = producer.get((s, v))
                            if pi is not None and self.finish[pi] + 1.0 <= t_now:
                                need[s] = v
                for s, v in need.items():
                    waited[s] = v
                    st.append(("wait", s, v))
                st.append(("op", o.fn, o.semname, o.inc))
            streams[e] = st
        self.streams = streams
        self.waited_final = None

    def final_wait(self, eng, semnames):
        for s in semnames:
            v = self.count.get(s, 0)
            if v > 0:
                self.streams[eng].append(("wait", s, v))

    def check(self):
        cnt = {}
        ptr = {e: 0 for e in self.streams}
        progress = True
        while progress:
            progress = False
            for e, st in self.streams.items():
                while ptr[e] < len(st):
                    ent = st[ptr[e]]
                    if ent[0] == "wait":
                        if cnt.get(ent[1], 0) < ent[2]:
                            break
                    else:
                        if ent[3] > 0:
                            cnt[ent[2]] = cnt.get(ent[2], 0) + ent[3]
                    ptr[e] += 1
                    progress = True
        for e, st in self.streams.items():
            assert ptr[e] == len(st), "deadlock in stream %s at %d/%d: %r" % (e, ptr[e], len(st), st[ptr[e]][:3])


class Rot:
    def __init__(self, n):
        self.n = n
        self.i = 0

    def alloc(self):
        r = self.i % self.n
        self.i += 1
        return r


def build_nc(NT=8, nstage=5):
    nc = bass.Bass("TRN2", target_bir_lowering=False)
    SL = NT * TT
    xT = nc.dram_tensor("xT", [D, SL], F32, kind="ExternalInput").ap()
    outT = nc.dram_tensor("outT", [D, SL], F32, kind="ExternalOutput").ap()
    par_d = nc.dram_tensor("par", [128, NPAR], F32, kind="ExternalInput").ap()
    sgu_d = nc.dram_tensor("sgu_bc", [128, 1024], F32, kind="ExternalInput").ap()
    bsp_d = nc.dram_tensor("bsp2", [2, 1024], F32, kind="ExternalInput").ap()
    wsT_d = nc.dram_tensor("wsT", [128, 1024], F32, kind="ExternalInput").ap()
    mask_d = nc.dram_tensor("maskT", [128, 128], F32, kind="ExternalInput").ap()
    wpool_d = nc.dram_tensor("wpool", [128, 512], F32, kind="ExternalInput").ap()
    w_d, scr = {}, {}
    for name, (nb, wd) in WSPEC.items():
        w_d[name] = nc.dram_tensor("w_" + name, [nb * 128, wd], F32, kind="ExternalInput").ap()
        scr[name] = nc.dram_tensor("s_" + name, [nb * 128, wd], BF16).ap()

    es = contextlib.ExitStack()

    def sb(name, shape, dt):
        return es.enter_context(nc.sbuf_tensor(name, shape, dt))

    with es:
        NH = 3
        hbuf = [sb("h%d" % i, [128, 8, TT], F32) for i in range(NH)]
        xn = sb("xn", [128, 8, TT], BF16)
        sqs = [sb("sq0", [128, 8, TT], BF16)]
        rstds = [sb("rstd0", [128, TT], F32), sb("rstd1", [128, TT], F32)]
        mix = sb("mix", [128, 8, TT], BF16)
        big = sb("big", [128, NF * TT], BF16)
        vn = sb("vn", [128, 4, 1024], BF16)
        gv = [sb("gv0", [128, 1024], F32), sb("gv1", [128, 1024], F32)]
        cv = sb("cv", [128, 4, TT + 2], F32)
        zb = sb("zb", [128, 4, TT + 15], F32)
        NWK = 6
        wk = [sb("wk%d" % i, [128, 528], F32) for i in range(NWK)]
        pl = [sb("pl%d" % i, [128, TT], BF16) for i in range(2)]
        ring = [sb("ring%d" % i, [128, RINGW], BF16) for i in range(NRING)]
        par = sb("par_sb", [128, NPAR], F32)
        sgu = sb("sgu_sb", [128, 1024], F32)
        bsp = sb("bsp", [2, 1024], BF16)
        wsT = sb("wsT_sb", [128, 1024], BF16)
        mask = sb("mask_sb", [128, 128], F32)
        wpool = sb("wpool_sb", [128, 512], BF16)
        onesM = sb("onesM", [128, 128], BF16)
        ones2 = sb("ones2", [2, 128], BF16)
        halo = [sb("halo0", [128, NF, 2], F32), sb("halo1", [128, NF, 2], F32)]
        ss = sb("ss", [128, 8], F32)
        junk = sb("junk", [128, TT], BF16)
        tmp16 = sb("tmp16", [128, 16], F32)
        psum = [es.enter_context(nc.psum_tensor("ps%d" % i, [128, TT], F32)) for i in range(8)]
        big_f = big[:, :].bitcast(F32)
        wsT_f = big_f[:, 0:1024]
        wpool_f = big_f[:, 1024:1536]
        bsp_f = big_f[0:2, 2048:3072]
        bsp_t = big_f[0:2, 3072:4096]
        bsp_hi = big[0:2, 8192:9216]
        BIGALL = [("big", f) for f in range(NF)]

        Sd = Sched()
        for e, ssync, win in [("pe", False, 1), ("act", True, 40), ("dve", True, 40), ("pool", True, 24), ("sp", False, 1)]:
            Sd.add_engine(e, ssync, win)
        PS = Rot(8)
        WK = Rot(NWK)
        PL = Rot(2)
        GV = Rot(2)
        state = {"nload": 0, "deferred": [], "nnorm": 0, "tile": 0}

        def P(name, i=0, n=1):
            o = PAR[name] + i
            return par[:, o:o + n]

        c0 = [
            Sd.op("sp", lambda e: e.dma_start(out=par[:, :], in_=par_d[:, :]), writes=["par"], sem="c0", inc=16, dur=3.0, occ=0.06),
            Sd.op("sp", lambda e: e.dma_start(out=sgu[:, :], in_=sgu_d[:, :]), writes=["sgu"], sem="c0", inc=16, dur=4.0, occ=0.06),
            Sd.op("sp", lambda e: e.dma_start(out=bsp_f, in_=bsp_d[:, :]), writes=["bsp_f"] + BIGALL, sem="c0", inc=16, dur=3.0, occ=0.06),
            Sd.op("sp", lambda e: e.dma_start(out=wsT_f, in_=wsT_d[:, :]), writes=["wsT_f"], sem="c0", inc=16, dur=4.0, occ=0.06),
            Sd.op("sp", lambda e: e.dma_start(out=mask[:, :], in_=mask_d[:, :]), writes=["mask"], sem="c0", inc=16, dur=3.0, occ=0.06),
            Sd.op("sp", lambda e: e.dma_start(out=wpool_f, in_=wpool_d[:, :]), writes=["wpool_f"], sem="c0", inc=16, dur=3.0, occ=0.06),
        ]
        Sd.group_done(c0)

        def x_load(tile):
            hb = tile % NH
            src_ap = xT.rearrange("(c p) t -> p c t", p=128)[:, :, tile * TT:(tile + 1) * TT]
            Sd.op("pool", lambda e: e.dma_start(out=hbuf[hb][:, :, :], in_=src_ap),
                  writes=[("h", hb, c) for c in range(8)], sem="xl%d" % hb, inc=16, dur=14.0, occ=0.5)

        def store(tile):
            hb = tile % NH
            dst = outT.rearrange("(c p) t -> p c t", p=128)[:, :, tile * TT:(tile + 1) * TT]
            Sd.op("pool", lambda e: e.dma_start(out=dst, in_=hbuf[hb][:, :, :]),
                  reads=[("h", hb, c) for c in range(8)], sem="st%d" % hb, inc=16, dur=14.0, occ=0.5)

        x_load(0)
        for t in range(1, min(NH, NT)):
            state["deferred"].append((2 * t, lambda t=t: x_load(t)))
        cast_t = [10.0]

        def cast_group(name, blocks, sem):
            ids = []
            for b in blocks:
                cast_t[0] += 7.0
                ids.append(Sd.op("pool", lambda e, name=name, b=b: e.dma_start(
                    out=scr[name][b * 128:(b + 1) * 128, :], in_=w_d[name][b * 128:(b + 1) * 128, :],
                    max_dma_last_dim=8192),
                    writes=[("scr", name, b)], sem=sem, inc=16, dur=cast_t[0], occ=0.01))
            Sd.group_done(ids)

        ncast = [0]

        def cast_tensor(name, per=4):
            nb = WSPEC[name][0]
            for b0 in range(0, nb, per):
                cast_group(name, list(range(b0, min(nb, b0 + per))), "cast%d" % ncast[0])
                ncast[0] += 1

        if not DIRECT_FIRST:
            for b in (3, 2, 1, 0):
                cast_group("ine", [b], "cast%d" % ncast[0])
                ncast[0] += 1
            for name in ["oute", "gu0", "dn0", "ino", "outo", "gu1", "dn1"]:
                cast_tensor(name)

        Sd.op("dve", lambda e: e.memset(onesM[:, :], 1.0 / 1024.0), writes=["onesM"])
        Sd.op("dve", lambda e: e.memset(ones2[:, :], 1.0), writes=["ones2"])
        Sd.op("dve", lambda e: e.memset(cv[:, :, :], 0.0), writes=[("cv", j) for j in range(4)])
        Sd.op("dve", lambda e: e.memset(zb[:, :, :], 0.0), writes=[("zb", j) for j in range(4)])
        for l in range(2):
            Sd.op("dve", lambda e, l=l: e.memset(halo[l][:, :, :], 0.0), writes=[("halo", l, f) for f in range(NF)])
        Sd.op("dve", lambda e: e.tensor_copy(out=wpool[:, :], in_=wpool_f), reads=["wpool_f"] + BIGALL, writes=["wpool"])
        for hd in range(8):
            Sd.op("dve", lambda e, hd=hd: e.tensor_tensor(
                out=wsT[:, hd * 128:(hd + 1) * 128], in0=wsT_f[:, hd * 128:(hd + 1) * 128], in1=mask[:, :], op=ALU.mult),
                reads=["wsT_f", "mask"] + BIGALL, writes=["wsT"])
        Sd.op("dve", lambda e: e.tensor_copy(out=bsp_hi, in_=bsp_f), reads=["bsp_f"] + BIGALL, writes=["bsp_hi"])
        Sd.op("dve", lambda e: e.tensor_tensor(out=bsp_t, in0=bsp_f, in1=bsp_hi, op=ALU.subtract),
              reads=["bsp_f", "bsp_hi"] + BIGALL, writes=["bsp_t"])
        Sd.op("dve", lambda e: e.tensor_copy(out=bsp[:, :], in_=bsp_t), reads=["bsp_t"] + BIGALL, writes=["bsp"])
        Sd.op("dve", lambda e: e.tensor_copy(out=bsp[0:1, :], in_=bsp_hi[0:1, :]), reads=["bsp_hi", "bsp"] + BIGALL, writes=["bsp"])

        def load_block(name, b):
            slot = state["nload"] % NRING
            state["nload"] += 1
            wd = WSPEC[name][1]
            if DIRECT_FIRST and state["tile"] == 0:
                Sd.op("pool", lambda e: e.dma_start(out=ring[slot][:, 0:wd], in_=w_d[name][b * 128:(b + 1) * 128, :],
                                                    max_dma_last_dim=8192),
                      writes=[("ring", slot)], sem="ringq%d" % slot, inc=16, dur=9.0, occ=0.6)
                Sd.op("sp", lambda e: e.dma_start(out=scr[name][b * 128:(b + 1) * 128, :], in_=ring[slot][:, 0:wd]),
                      reads=[("ring", slot)], writes=[("scr", name, b)], sem="wb%d" % slot, inc=16, dur=6.0, occ=0.06)
            else:
                Sd.op("sp", lambda e: e.dma_start(out=ring[slot][:, 0:wd], in_=scr[name][b * 128:(b + 1) * 128, :]),
                      reads=[("scr", name, b)], writes=[("ring", slot)], sem="ring%d" % slot, inc=16, dur=8.0, occ=0.06)
            nd = []
            for cnt, f in state["deferred"]:
                if cnt <= 1:
                    f()
                else:
                    nd.append((cnt - 1, f))
            state["deferred"] = nd
            return slot

        def mm_group(ps_i, mms, out_ap=None):
            n = len(mms)
            o = psum[ps_i][:, :] if out_ap is None else out_ap
            for i, (lhsT, rhs, reads) in enumerate(mms):
                Sd.op("pe", lambda e, lhsT=lhsT, rhs=rhs, i=i: e.matmul(o, lhsT, rhs, start=(i == 0), stop=(i == n - 1)),
                      reads=reads, writes=[("ps", ps_i)] if i == 0 else [], inc=1 if i == n - 1 else 0)

        def mm_groups_il(groups):
            n = len(groups[0][1])
            ids = [[] for _ in groups]
            for i in range(n):
                for gi, (ps_i, mms) in enumerate(groups):
                    lhsT, rhs, reads = mms[i]
                    o = psum[ps_i][:, :]
                    ids[gi].append(Sd.op("pe", lambda e, o=o, lhsT=lhsT, rhs=rhs, i=i: e.matmul(o, lhsT, rhs, start=(i == 0), stop=(i == n - 1)),
                                         reads=reads, writes=[("ps", ps_i)] if i == 0 else [], inc=1 if i == n - 1 else 0, manual=True))
            for g in ids:
                for i in g:
                    Sd.ops[i].done = g[-1]

        def rmsnorm(hb, gname, out_inplace=False):
            h = hbuf[hb]
            nb = state["nnorm"] % 2
            state["nnorm"] += 1
            sq = sqs[0]
            rstd = rstds[nb]
            for c in range(8):
                Sd.op("act", lambda e, c=c: e.activation(out=sq[:, c, :], in_=h[:, c, :], func=AF.Square),
                      reads=[("h", hb, c)], writes=[("sq", 0, c)])
            for c in range(4):
                Sd.op("dve", lambda e, c=c: e.tensor_tensor(out=sq[:, c, :], in0=sq[:, c, :], in1=sq[:, c + 4, :], op=ALU.add),
                      reads=[("sq", 0, c), ("sq", 0, c + 4)], writes=[("sq", 0, c)], dur=0.5)
            p = PS.alloc()
            mm_group(p, [(onesM[:, :], sq[:, c, :], ["onesM", ("sq", 0, c)]) for c in range(4)])
            Sd.op("act", lambda e: e.activation(out=rstd[:, :], in_=psum[p][:, :], func=AF.Ln, bias=P("eps")),
                  reads=[("ps", p), "par"], writes=[("rstd", nb)], table="lnexp")
            Sd.op("act", lambda e: e.activation(out=rstd[:, :], in_=rstd[:, :], func=AF.Exp, scale=-0.5),
                  reads=[("rstd", nb)], writes=[("rstd", nb)], table="lnexp")
            for c in range(8):
                if out_inplace:
                    Sd.op("dve", lambda e, c=c: e.scalar_tensor_tensor(
                        out=h[:, c, :], in0=h[:, c, :], scalar=P(gname, c), in1=rstd[:, :], op0=ALU.mult, op1=ALU.mult),
                        reads=[("h", hb, c), ("rstd", nb), "par"], writes=[("h", hb, c)], lag=300)
                else:
                    Sd.op("dve", lambda e, c=c: e.scalar_tensor_tensor(
                        out=xn[:, c, :], in0=h[:, c, :], scalar=P(gname, c), in1=rstd[:, :], op0=ALU.mult, op1=ALU.mult),
                        reads=[("h", hb, c), ("rstd", nb), "par"], writes=[("xn", c)])

        def resid_add(hb, oc, p):
            h = hbuf[hb]
            Sd.op("dve", lambda e: e.tensor_tensor(out=h[:, oc, :], in0=h[:, oc, :], in1=psum[p][:, :], op=ALU.add),
                  reads=[("h", hb, oc), ("ps", p)], writes=[("h", hb, oc)])

        def out_proj(hb, name, kcs=tuple(range(8))):
            for blk in range(2):
                slot = load_block(name, blk)
                for ol in range(4):
                    oc = blk * 4 + ol
                    p = PS.alloc()
                    mm_group(p, [(ring[slot][:, kc * 512 + ol * 128: kc * 512 + (ol + 1) * 128], mix[:, kc, :],
                                  [("ring", slot), ("mix", kc)]) for kc in kcs])
                    resid_add(hb, oc, p)

        def mixer0_j(hb, tile, j, il=False, pre_tail=None):
            slot = load_block("ine", j)
            pidx = {}
            grps = []
            for q in (3, 0, 1, 2):
                p = PS.alloc()
                pidx[q] = p
                grps.append((p, [(ring[slot][:, kc * 512 + q * 128: kc * 512 + (q + 1) * 128], xn[:, kc, :],
                                  [("ring", slot), ("xn", kc)]) for kc in range(8)]))
            if il:
                mm_groups_il(grps)
                if pre_tail is not None:
                    pre_tail()
            else:
                for gi, (p, mms) in enumerate(grps):
                    mm_group(p, mms)
                    if gi == 1 and pre_tail is not None:
                        pre_tail()
            pc, pv, pb, pz = pidx[0], pidx[1], pidx[2], pidx[3]
            t1 = WK.alloc()
            Sd.op("act", lambda e: e.activation(out=wk[t1][:, 0:TT], in_=psum[pc][:, :], func=AF.Copy),
                  reads=[("ps", pc)], writes=[("wk", t1)])
            Sd.op("dve", lambda e: e.tensor_tensor(out=cv[:, j, 2:TT + 2], in0=wk[t1][:, 0:TT], in1=psum[pv][:, :], op=ALU.mult),
                  reads=[("wk", t1), ("ps", pv)], writes=[("cv", j)])
            y = WK.alloc()
            ca = PAR["conv_a"]
            Sd.op("dve", lambda e: e.tensor_scalar(out=wk[y][:, 0:TT], in0=cv[:, j, 0:TT], scalar1=par[:, ca + j: ca + j + 1],
                                                    scalar2=None, op0=ALU.mult),
                  reads=[("cv", j), "par"], writes=[("wk", y)])
            for k in (1, 2):
                Sd.op("dve", lambda e, k=k: e.scalar_tensor_tensor(
                    out=wk[y][:, 0:TT], in0=cv[:, j, k:TT + k], scalar=par[:, ca + k * 4 + j: ca + k * 4 + j + 1],
                    in1=wk[y][:, 0:TT], op0=ALU.mult, op1=ALU.add),
                    reads=[("cv", j), ("wk", y), "par"], writes=[("wk", y)])
            Sd.op("dve", lambda e: e.tensor_tensor(out=mix[:, j, :], in0=wk[y][:, 0:TT], in1=psum[pb][:, :], op=ALU.mult),
                  reads=[("wk", y), ("ps", pb)], writes=[("mix", j)])
            Sd.op("dve", lambda e: e.tensor_copy(out=cv[:, j, 0:2], in_=cv[:, j, TT:TT + 2]),
                  reads=[("cv", j)], writes=[("cv", j)], dur=0.1)
            Sd.op("act", lambda e: e.activation(out=zb[:, j, 15:TT + 15], in_=psum[pz][:, :], func=AF.Copy),
                  reads=[("ps", pz)], writes=[("zb", j)])
            src = zb[:, j, :]
            skey = ("zb", j)
            sh = 1
            for lvl in range(j + 1):
                d = WK.alloc()
                lo = 2 * sh - 1
                Sd.op("dve", lambda e, src=src, d=d, lo=lo, sh=sh: e.tensor_tensor(
                    out=wk[d][:, lo:TT + 15], in0=src[:, lo:TT + 15], in1=src[:, lo - sh:TT + 15 - sh], op=ALU.add),
                    reads=[skey], writes=[("wk", d)])
                src = wk[d]
                skey = ("wk", d)
                sh *= 2
            w = 2 ** (j + 1)
            pb_i = PL.alloc()
            fsrc = src
            Sd.op("dve", lambda e: e.scalar_tensor_tensor(
                out=pl[pb_i][:, :], in0=fsrc[:, 15:TT + 15], scalar=1.0 / w, in1=zb[:, j, 15:TT + 15],
                op0=ALU.mult, op1=ALU.subtract),
                reads=[skey, ("zb", j)], writes=[("pl", pb_i)])
            if tile == 0:
                rc = PAR["rc"] + j * 16
                Sd.op("dve", lambda e: e.tensor_tensor(out=tmp16[:, :], in0=fsrc[:, 15:31], in1=par[:, rc:rc + 16], op=ALU.mult),
                      reads=[skey, "par"], writes=["tmp16"], dur=0.1)
                Sd.op("dve", lambda e: e.tensor_tensor(out=pl[pb_i][:, 0:16], in0=tmp16[:, :], in1=zb[:, j, 15:31], op=ALU.subtract),
                      reads=["tmp16", ("zb", j), ("pl", pb_i)], writes=[("pl", pb_i)], dur=0.1)
            Sd.op("dve", lambda e: e.tensor_copy(out=zb[:, j, 0:15], in_=zb[:, j, TT:TT + 15]),
                  reads=[("zb", j)], writes=[("zb", j)], dur=0.1)
            def tail():
                pp = PS.alloc()
                mm_group(pp, [(wpool[:, j * 128:(j + 1) * 128], pl[pb_i][:, :], ["wpool", ("pl", pb_i)])])
                psc = PAR["pool_scale"] + j
                Sd.op("act", lambda e: e.activation(out=mix[:, 4 + j, :], in_=psum[pp][:, :], func=AF.Copy, scale=par[:, psc:psc + 1]),
                      reads=[("ps", pp), "par"], writes=[("mix", 4 + j)])
            return tail

        def mixer0(hb, tile, hook):
            tl = None
            for j in (3, 2, 1, 0):
                tl = mixer0_j(hb, tile, j, il=False, pre_tail=tl)
            tl()
            hook()
            out_proj(hb, "oute", kcs=(1, 2, 3, 5, 6, 7, 0, 4))

        def ffn_grps(slot, half):
            pg = PS.alloc()
            gg = (pg, [(ring[slot][:, kc * 256 + half * 128: kc * 256 + (half + 1) * 128], xn[:, kc, :],
                        [("ring", slot), ("xn", kc)]) for kc in range(8)])
            pu = PS.alloc()
            gu = (pu, [(ring[slot][:, (8 + kc) * 256 + half * 128: (8 + kc) * 256 + (half + 1) * 128], xn[:, kc, :],
                        [("ring", slot), ("xn", kc)]) for kc in range(8)])
            return gg, gu

        def ffn_f(l, f, pg, pu):
            cf = PAR["conv_ffn%d" % l]
            bf = PAR["b_ffn%d" % l]
            w_ = WK.alloc()
            g_ = WK.alloc()
            Sd.op("dve", lambda e: e.tensor_copy(out=wk[w_][:, 0:2], in_=halo[l][:, f, :]),
                  reads=[("halo", l, f)], writes=[("wk", w_)], dur=0.1)
            Sd.op("act", lambda e: e.activation(out=wk[w_][:, 2:TT + 2], in_=psum[pg][:, :], func=AF.Copy),
                  reads=[("ps", pg), ("wk", w_)], writes=[("wk", w_)])
            Sd.op("act", lambda e: e.activation(
                out=wk[g_][:, 0:TT], in_=psum[pg][:, :], func=AF.Identity,
                scale=par[:, cf + 2 * NF + f: cf + 2 * NF + f + 1], bias=par[:, bf + f: bf + f + 1]),
                reads=[("ps", pg), "par"], writes=[("wk", g_)])
            for k in (1, 0):
                Sd.op("dve", lambda e, k=k: e.scalar_tensor_tensor(
                    out=wk[g_][:, 0:TT], in0=wk[w_][:, k:TT + k], scalar=par[:, cf + k * NF + f: cf + k * NF + f + 1],
                    in1=wk[g_][:, 0:TT], op0=ALU.mult, op1=ALU.add),
                    reads=[("wk", w_), ("wk", g_), "par"], writes=[("wk", g_)])
            Sd.op("dve", lambda e: e.tensor_copy(out=halo[l][:, f, :], in_=wk[w_][:, TT:TT + 2]),
                  reads=[("wk", w_)], writes=[("halo", l, f)], dur=0.1)
            Sd.op("act", lambda e: e.activation(out=wk[g_][:, 0:TT], in_=wk[g_][:, 0:TT], func=AF.Silu),
                  reads=[("wk", g_)], writes=[("wk", g_)], table="silu")
            Sd.op("dve", lambda e: e.tensor_tensor(
                out=big[:, f * TT:(f + 1) * TT], in0=wk[g_][:, 0:TT], in1=psum[pu][:, :], op=ALU.mult),
                reads=[("wk", g_), ("ps", pu)], writes=[("big", f)])

        def ffn_down(l, hb, oc):
            slot = load_block("dn%d" % l, oc)
            p = PS.alloc()
            mm_group(p, [(ring[slot][:, kc * 128:(kc + 1) * 128], big[:, kc * TT:(kc + 1) * TT],
                          [("ring", slot), ("big", kc)]) for kc in range(NF)])
            resid_add(hb, oc, p)

        def ffn(l, hb, hook):
            for blk in range(11):
                slot = load_block("gu%d" % l, blk)
                for half in range(2):
                    gg, gu = ffn_grps(slot, half)
                    mm_group(*gg)
                    mm_group(*gu)
                    ffn_f(l, blk * 2 + half, gg[0], gu[0])
            hook()
            for oc in range(8):
                ffn_down(l, hb, oc)

        def m1_u(slot, ol, c):
            p = PS.alloc()
            mm_group(p, [(ring[slot][:, kc * 512 + ol * 128: kc * 512 + (ol + 1) * 128], xn[:, kc, :],
                          [("ring", slot), ("xn", kc)]) for kc in range(8)])
            Sd.op("act", lambda e: e.activation(out=big_f[:, c * TT:(c + 1) * TT], in_=psum[p][:, :], func=AF.Gelu),
                  reads=[("ps", p)], writes=[("big", 2 * c), ("big", 2 * c + 1)], table="gelu")

        def m1_v_grp(slot, ts):
            p = PS.alloc()
            return (p, [(xn[:, kc, ts * 128:(ts + 1) * 128], ring[slot][:, kc * 512:(kc + 1) * 512],
                         [("ring", slot), ("xn", kc)]) for kc in range(8)])

        def m1_v_half(p, g, half):
            Sd.op("act", lambda e: e.activation(
                out=gv[g][:, half * 512:(half + 1) * 512], in_=psum[p][:, :], func=AF.Gelu),
                reads=[("ps", p)], writes=[("gv", g, half)], table="gelu")
            Sd.op("act", lambda e: e.activation(
                out=junk[:, :], in_=gv[g][:, half * 512:(half + 1) * 512], func=AF.Square, scale=1.0 / 32.0,
                accum_out=ss[:, half:half + 1]),
                reads=[("gv", g, half)], writes=[("ss", half), "junk"])

        def m1_v(ps2, ts):
            g = GV.alloc()
            for half in range(2):
                m1_v_half(ps2[half], g, half)
            Sd.op("dve", lambda e: e.tensor_scalar(out=ss[:, 2:3], in0=ss[:, 0:1], scalar1=ss[:, 1:2], scalar2=EPS,
                                                    op0=ALU.add, op1=ALU.add),
                  reads=[("ss", 0), ("ss", 1)], writes=[("ss", 2)], dur=0.1)
            Sd.op("act", lambda e: e.activation(out=ss[:, 4:5], in_=ss[:, 2:3], func=AF.Ln),
                  reads=[("ss", 2)], writes=[("ss", 4)], dur=0.25, table="lnexp", urgent=True)
            Sd.op("act", lambda e: e.activation(out=ss[:, 3:4], in_=ss[:, 4:5], func=AF.Exp, scale=-0.5),
                  reads=[("ss", 4)], writes=[("ss", 3)], dur=0.25, table="lnexp", urgent=True)
            for half in range(2):
                Sd.op("dve", lambda e, half=half: e.scalar_tensor_tensor(
                    out=vn[:, ts, half * 512:(half + 1) * 512], in0=gv[g][:, half * 512:(half + 1) * 512],
                    scalar=ss[:, 3:4], in1=sgu[:, half * 512:(half + 1) * 512], op0=ALU.mult, op1=ALU.mult),
                    reads=[("gv", g, half), ("ss", 3), "sgu"], writes=[("vn", ts)])

        def m1_gate(hd):
            p = PS.alloc()
            for ts in range(4):
                o = psum[p][:, ts * 128:(ts + 1) * 128]
                Sd.op("pe", lambda e, o=o, ts=ts: e.matmul(o, vn[:, ts, hd * 128:(hd + 1) * 128], wsT[:, hd * 128:(hd + 1) * 128],
                                                           start=True, stop=False),
                      reads=[("vn", ts), "wsT"], writes=[("ps", p)] if ts == 0 else [], inc=0, dur=0.25, occ=0.08)
                Sd.op("pe", lambda e, o=o: e.matmul(o, ones2[0:2, :], bsp[0:2, hd * 128:(hd + 1) * 128], start=False, stop=True),
                      reads=["ones2", "bsp"], writes=[], inc=1 if ts == 3 else 0, dur=0.25, occ=0.06)
            Sd.op("dve", lambda e: e.tensor_tensor(out=mix[:, hd, :], in0=big_f[:, hd * TT:(hd + 1) * TT], in1=psum[p][:, :], op=ALU.mult),
                  reads=[("big", 2 * hd), ("big", 2 * hd + 1), ("ps", p)], writes=[("mix", hd)])

        def mixer1(hb, hook):
            slots = [load_block("ino", 2), load_block("ino", 3)]
            for ts in range(4):
                g2 = [m1_v_grp(slots[half], ts) for half in (0, 1)]
                for p, mms in g2:
                    mm_group(p, mms)
                m1_v([g2[0][0], g2[1][0]], ts)
            for blk in range(2):
                slot = load_block("ino", blk)
                for ol in range(4):
                    m1_u(slot, ol, blk * 4 + ol)
            for hd in range(8):
                m1_gate(hd)
            hook()
            out_proj(hb, "outo")

        stage_names = ["m0", "f0", "m1", "f1"][:min(nstage, 4)]
        norm_of = {"m0": "norm_mix0", "f0": "norm_ffn0", "m1": "norm_mix1", "f1": "norm_ffn1"}
        bodies = []
        for pr in range(0, NT, 2):
            tiles = [t for t in (pr, pr + 1) if t < NT]
            for s in stage_names:
                for t in tiles:
                    bodies.append((s, t))
        pending_fin = []

        def fin(t):
            if nstage >= 5:
                rmsnorm(t % NH, "final_norm", out_inplace=True)
            store(t)
            if t + NH < NT:
                x_load(t + NH)

        def make_hook(i):
            def hook():
                while pending_fin:
                    fin(pending_fin.pop(0))
                if i + 1 < len(bodies):
                    s2, t2 = bodies[i + 1]
                    rmsnorm(t2 % NH, norm_of[s2])
            return hook

        s0, t0 = bodies[0]
        rmsnorm(t0 % NH, norm_of[s0])
        for i, (s, t) in enumerate(bodies):
            hb = t % NH
            state["tile"] = t
            hk = make_hook(i)
            if s == "m0":
                mixer0(hb, t, hk)
            elif s == "f0":
                ffn(0, hb, hk)
            elif s == "m1":
                mixer1(hb, hk)
            else:
                ffn(1, hb, hk)
            if s == stage_names[-1]:
                pending_fin.append(t)
        while pending_fin:
            fin(pending_fin.pop(0))
        Sd.schedule()
        Sd.build_streams()
        Sd.final_wait("sp", ["st%d" % i for i in range(NH)])
        Sd.check()

        semnames = sorted(Sd.count.keys())
        sems = {n: es.enter_context(nc.semaphore("s_" + n)) for n in semnames}
        with nc.Block() as block:
            def replay(name, eng):
                pend = []
                for ent in Sd.streams[name]:
                    if ent[0] == "wait":
                        pend.append((ent[1], ent[2]))
                    else:
                        _, fn, sn, inc = ent
                        for s, v in (pend[:-1] if FUSE_WAIT else pend):
                            eng.wait_ge(sems[s], v)
                        ins = fn(eng)
                        if pend and FUSE_WAIT:
                            ins._wait_ge(sems[pend[-1][0]], pend[-1][1])
                        pend = []
                        if inc > 0:
                            ins.then_inc(sems[sn], inc)
                for s, v in pend:
                    eng.wait_ge(sems[s], v)

            @block.tensor
            def _(e):
                replay("pe", e)

            @block.scalar
            def _(e):
                replay("act", e)

            @block.vector
            def _(e):
                replay("dve", e)

            @block.gpsimd
            def _(e):
                replay("pool", e)

            @block.sync
            def _(e):
                replay("sp", e)
    return nc


def _fm(v):
    v = np.asarray(v, np.float32)
    return np.ascontiguousarray(v.reshape(-1, 128).T)


def _blk_cols(w, cols_per_blk):
    K, N = w.shape
    nb = N // cols_per_blk
    a = w.reshape(K // 128, 128, nb, cols_per_blk).transpose(2, 1, 0, 3)
    return np.ascontiguousarray(a.reshape(nb * 128, (K // 128) * cols_per_blk))


def prep_shared(inp):
    f = lambda k: np.asarray(inp[k], np.float32)
    par = np.zeros((128, NPAR), np.float32)

    def put(name, arr):
        arr = np.asarray(arr, np.float32).reshape(128, -1)
        par[:, PAR[name]:PAR[name] + arr.shape[1]] = arr
    put("norm_mix0", _fm(f("norm_mix")[0]))
    put("norm_mix1", _fm(f("norm_mix")[1]))
    put("norm_ffn0", _fm(f("norm_ffn")[0]))
    put("norm_ffn1", _fm(f("norm_ffn")[1]))
    put("final_norm", _fm(f("final_norm")))
    ca = f("conv_a")[0]
    put("conv_a", np.concatenate([_fm(ca[k]) for k in range(3)], axis=1))
    put("pool_scale", _fm(f("pool_scale")[0]))
    for l in range(2):
        cfw = f("conv_ffn")[l]
        put("conv_ffn%d" % l, np.concatenate([_fm(cfw[k]) for k in range(3)], axis=1))
        put("b_ffn%d" % l, _fm(f("b_conv_ffn")[l]))
    rc = np.zeros((4, 16), np.float32)
    for j in range(4):
        w = 2 ** (j + 1)
        rc[j] = 1.0 / np.minimum(np.arange(1, 17), w)
    put("rc", np.broadcast_to(rc.reshape(1, 64), (128, 64)))
    put("eps", np.full((128, 1), EPS, np.float32))

    sh = {"par": par}
    sh["sgu_bc"] = np.ascontiguousarray(np.broadcast_to(f("sgu_norm")[0][None, :], (128, 1024)))
    sh["bsp2"] = np.ascontiguousarray(np.broadcast_to(f("b_spatial")[0].reshape(1, 1024), (2, 1024)))
    ws = f("w_spatial")[0]
    sh["wsT"] = np.ascontiguousarray(ws.transpose(2, 0, 1).reshape(128, 1024))
    sh["maskT"] = np.ascontiguousarray(np.triu(np.ones((128, 128), np.float32)))
    sh["wpool"] = np.ascontiguousarray(f("w_pool")[0].transpose(1, 0, 2).reshape(128, 512))

    wie = f("w_in_even")[0]
    cols = []
    for j in range(4):
        for base in (512, 1024, 0, 1536):
            cols.append(np.arange(base + j * 128, base + (j + 1) * 128))
    sh["w_ine"] = _blk_cols(wie[:, np.concatenate(cols)], 512)
    sh["w_oute"] = _blk_cols(f("w_out_even")[0], 512)
    sh["w_ino"] = _blk_cols(f("w_in_odd")[0], 512)
    sh["w_outo"] = _blk_cols(f("w_out_odd")[0], 512)
    for l in range(2):
        g = f("w_ffn_gate")[l].reshape(8, 128, 11, 256)
        u = f("w_ffn_up")[l].reshape(8, 128, 11, 256)
        gu = np.stack([g, u], axis=0).transpose(3, 2, 0, 1, 4)
        sh["w_gu%d" % l] = np.ascontiguousarray(gu.reshape(11 * 128, 4096))
        sh["w_dn%d" % l] = _blk_cols(f("w_ffn_down")[l], 128)
    return sh


_NC_CACHE = {}


def kernel(**inputs):
    x = np.asarray(inputs["x"], np.float32)
    B = x.shape[0]
    sh = prep_shared(inputs)
    if "nc" not in _NC_CACHE:
        _NC_CACHE["nc"] = build_nc(NT=S // TT, nstage=5)
    nc = _NC_CACHE["nc"]
    in_maps = []
    for b in range(B):
        m = dict(sh)
        m["xT"] = np.ascontiguousarray(x[b].T)
        in_maps.append(m)
    res = run_bass_kernel_spmd(nc, in_maps, core_ids=list(range(B)))
    out = np.stack([np.ascontiguousarray(r["outT"].T) for r in res.results], axis=0)
    return out.astype(np.float32)
```

```python
import contextlib
import numpy as np
import concourse.bass as bass
import concourse.mybir as mybir
from concourse.bass_utils import run_bass_kernel_spmd

F32 = mybir.dt.float32
BF16 = mybir.dt.bfloat16
AF = mybir.ActivationFunctionType
ALU = mybir.AluOpType

D = 1024
S = 4096
TT = 512
DFF = 2816
NF = DFF // 128
EPS = 1e-6
NRING = 6
FUSE_WAIT = False
DIRECT_FIRST = True
RINGW = 4096

PAR = {}
_off = 0
for _name, _n in [("norm_mix0", 8), ("norm_mix1", 8), ("norm_ffn0", 8), ("norm_ffn1", 8),
                  ("final_norm", 8), ("conv_a", 12), ("pool_scale", 4),
                  ("conv_ffn0", 66), ("conv_ffn1", 66), ("b_ffn0", 22), ("b_ffn1", 22),
                  ("rc", 64), ("eps", 1)]:
    PAR[_name] = _off
    _off += _n
NPAR = _off

WSPEC = {
    "ine": (4, 4096), "oute": (2, 4096), "gu0": (11, 4096), "dn0": (8, 2816),
    "ino": (4, 4096), "outo": (2, 4096), "gu1": (11, 4096), "dn1": (8, 2816),
}


class Op:
    __slots__ = ("idx", "eng", "fn", "deps", "semname", "inc", "dur", "occ", "done", "value", "users", "prio", "table", "urgent")

    def __init__(self, idx, eng, fn, semname, inc, dur, occ):
        self.idx, self.eng, self.fn, self.semname, self.inc = idx, eng, fn, semname, inc
        self.dur, self.occ = dur, occ
        self.deps = set()
        self.done = idx
        self.prio = idx
        self.table = None
        self.urgent = False
        self.value = None
        self.users = []


class Sched:
    SEM_LAT = 0.12

    def __init__(self):
        self.ops = []
        self.engines = {}
        self.lw = {}
        self.rd = {}
        self.open_group = {}

    def add_engine(self, name, self_sync, window):
        self.engines[name] = dict(self_sync=self_sync, window=window, lookahead=0)
        self.open_group[name] = []

    DEFAULTS = {"pe": (0.40, 0.215), "act": (0.70, 0.70), "dve": (0.70, 0.70), "pool": (1.3, 1.3), "sp": (6.0, 0.06)}

    def op(self, eng, fn, reads=(), writes=(), sem=None, inc=1, dur=None, occ=None, lag=0, manual=False, table=None, urgent=False):
        semname = eng if sem is None else sem
        dd, do = self.DEFAULTS[eng]
        if dur is None:
            dur = dd
            if occ is None:
                occ = do
        elif occ is None:
            occ = dur
        o = Op(len(self.ops), eng, fn, semname, inc, dur, occ)
        o.prio = o.idx + lag
        o.table = table
        o.urgent = urgent
        self.ops.append(o)
        for k in reads:
            w = self.lw.get(k)
            if w is not None:
                o.deps.add(w)
        for k in writes:
            w = self.lw.get(k)
            if w is not None:
                o.deps.add(w)
            o.deps.update(self.rd.get(k, ()))
        o.deps.discard(o.idx)
        for k in reads:
            self.rd.setdefault(k, []).append(o.idx)
        for k in writes:
            self.lw[k] = o.idx
            self.rd[k] = []
        if manual:
            pass
        elif inc == 0:
            self.open_group[eng].append(o.idx)
        else:
            for i in self.open_group[eng]:
                self.ops[i].done = o.idx
            self.open_group[eng] = []
        return o.idx

    def group_done(self, idxs):
        last = max(idxs)
        for i in idxs:
            self.ops[i].done = last

    def schedule(self):
        ops = self.ops
        for o in ops:
            extra = set()
            for d in o.deps:
                dn = ops[d].done
                if dn != d and dn != o.idx:
                    assert dn < o.idx
                    extra.add(dn)
            o.deps |= extra
        for o in ops:
            for d in o.deps:
                ops[d].users.append(o.idx)
        queues = {e: [] for e in self.engines}
        for o in ops:
            queues[o.eng].append(o.idx)
        pos = {e: 0 for e in self.engines}
        sched = [False] * len(ops)
        finish = [0.0] * len(ops)
        free = {e: 0.0 for e in self.engines}
        order = {e: [] for e in self.engines}
        gorder = []
        cur_table = {e: None for e in self.engines}
        TBL = 1.3
        cand = {e: None for e in self.engines}
        dirty = {e: True for e in self.engines}
        remaining = len(ops)

        def best(e):
            q = queues[e]
            p = pos[e]
            while p < len(q) and sched[q[p]]:
                p += 1
            pos[e] = p
            W = self.engines[e]["window"]
            bst = None
            seen = 0
            i = p
            while i < len(q) and seen < W:
                oi = q[i]
                i += 1
                if sched[oi]:
                    continue
                seen += 1
                o = ops[oi]
                ok = True
                st = free[e]
                for d in o.deps:
                    if not sched[d]:
                        ok = False
                        break
                    t = finish[d] + self.SEM_LAT
                    if t > st:
                        st = t
                if not ok:
                    continue
                real = st
                if o.table is not None and o.table != cur_table[e]:
                    real = st + TBL
                    if not o.urgent:
                        st = real
                if bst is None or st < bst[0] - 1e-9 or (st < bst[0] + 1e-9 and o.prio < ops[bst[1]].prio):
                    bst = (st, oi, real)
            return bst

        while remaining:
            for e in self.engines:
                if dirty[e]:
                    cand[e] = best(e)
                    dirty[e] = False
            pick = None
            for e in self.engines:
                c = cand[e]
                if c is not None and (pick is None or c[0] < pick[0]):
                    pick = (c[0], c[1], e)
            assert pick is not None, "scheduler deadlock"
            st, oi, e = pick
            st = cand[e][2]
            o = ops[oi]
            if o.table is not None:
                cur_table[e] = o.table
            sched[oi] = True
            finish[oi] = st + o.dur
            free[e] = st + o.occ
            order[e].append(oi)
            gorder.append(oi)
            remaining -= 1
            dirty[e] = True
            for u in o.users:
                dirty[ops[u].eng] = True
        self.order = order
        self.gorder = gorder
        self.finish = finish
        self.start = {}
        for oi in gorder:
            self.start[oi] = finish[oi] - ops[oi].dur
        self.est_total = max(finish) if finish else 0.0

    def build_streams(self):
        ops = self.ops
        count = {}
        posn = {}
        for e, lst in self.order.items():
            for i, oi in enumerate(lst):
                posn[oi] = i
        for oi in self.gorder:
            o = ops[oi]
            if o.inc > 0:
                count[o.semname] = count.get(o.semname, 0) + o.inc
                o.value = count[o.semname]
        self.count = count
        streams = {}
        producer = {}
        for o in ops:
            if o.inc > 0:
                producer[(o.semname, o.value)] = o.idx
        for e, lst in self.order.items():
            st = []
            waited = {}
            ss = self.engines[e]["self_sync"]
            needs = []
            for oi in lst:
                o = ops[oi]
                need = {}
                for d in o.deps:
                    dn = ops[ops[d].done]
                    assert dn.value is not None
                    if dn.eng == e and dn.semname == e:
                        assert posn[d] < posn[oi]
                        if not ss:
                            continue
                        assert posn[dn.idx] < posn[oi], "same-engine wait on a later op"
                    if need.get(dn.semname, 0) < dn.value:
                        need[dn.semname] = dn.value
                needs.append(need)
            look = self.engines[e].get("lookahead", 0)
            for i, oi in enumerate(lst):
                o = ops[oi]
                need = {s: v for s, v in needs[i].items() if waited.get(s, 0) < v}
                if need and look:
                    t_now = self.start[oi]
                    for j in range(i + 1, min(len(lst), i + 1 + look)):
                        for s, v in needs[j].items():
                            if waited.get(s, 0) >= v or need.get(s, 0) >= v:
                                continue
                            pi = producer.get((s, v))
                            if pi is not None and self.finish[pi] + 1.0 <= t_now:
                                need[s] = v
                for s, v in need.items():
                    waited[s] = v
                    st.append(("wait", s, v))
                st.append(("op", o.fn, o.semname, o.inc))
            streams[e] = st
        self.streams = streams
        self.waited_final = None

    def final_wait(self, eng, semnames):
        for s in semnames:
            v = self.count.get(s, 0)
            if v > 0:
                self.streams[eng].append(("wait", s, v))

    def check(self):
        cnt = {}
        ptr = {e: 0 for e in self.streams}
        progress = True
        while progress:
            progress = False
            for e, st in self.streams.items():
                while ptr[e] < len(st):
                    ent = st[ptr[e]]
                    if ent[0] == "wait":
                        if cnt.get(ent[1], 0) < ent[2]:
                            break
                    else:
                        if ent[3] > 0:
                            cnt[ent[2]] = cnt.get(ent[2], 0) + ent[3]
                    ptr[e] += 1
                    progress = True
        for e, st in self.streams.items():
            assert ptr[e] == len(st), "deadlock in stream %s at %d/%d: %r" % (e, ptr[e], len(st), st[ptr[e]][:3])


class Rot:
    def __init__(self, n):
        self.n = n
        self.i = 0

    def alloc(self):
        r = self.i % self.n
        self.i += 1
        return r


def build_nc(NT=8, nstage=5):
    nc = bass.Bass("TRN2", target_bir_lowering=False)
    SL = NT * TT
    xT = nc.dram_tensor("xT", [D, SL], F32, kind="ExternalInput").ap()
    outT = nc.dram_tensor("outT", [D, SL], F32, kind="ExternalOutput").ap()
    par_d = nc.dram_tensor("par", [128, NPAR], F32, kind="ExternalInput").ap()
    sgu_d = nc.dram_tensor("sgu_bc", [128, 1024], F32, kind="ExternalInput").ap()
    bsp_d = nc.dram_tensor("bsp2", [2, 1024], F32, kind="ExternalInput").ap()
    wsT_d = nc.dram_tensor("wsT", [128, 1024], F32, kind="ExternalInput").ap()
    mask_d = nc.dram_tensor("maskT", [128, 128], F32, kind="ExternalInput").ap()
    wpool_d = nc.dram_tensor("wpool", [128, 512], F32, kind="ExternalInput").ap()
    w_d, scr = {}, {}
    for name, (nb, wd) in WSPEC.items():
        w_d[name] = nc.dram_tensor("w_" + name, [nb * 128, wd], F32, kind="ExternalInput").ap()
        scr[name] = nc.dram_tensor("s_" + name, [nb * 128, wd], BF16).ap()

    es = contextlib.ExitStack()

    def sb(name, shape, dt):
        return es.enter_context(nc.sbuf_tensor(name, shape, dt))

    with es:
        NH = 3
        hbuf = [sb("h%d" % i, [128, 8, TT], F32) for i in range(NH)]
        xn = sb("xn", [128, 8, TT], BF16)
        sqs = [sb("sq0", [128, 8, TT], BF16)]
        rstds = [sb("rstd0", [128, TT], F32), sb("rstd1", [128, TT], F32)]
        mix = sb("mix", [128, 8, TT], BF16)
        big = sb("big", [128, NF * TT], BF16)
        vn = sb("vn", [128, 4, 1024], BF16)
        gv = [sb("gv0", [128, 1024], F32), sb("gv1", [128, 1024], F32)]
        cv = sb("cv", [128, 4, TT + 2], F32)
        zb = sb("zb", [128, 4, TT + 15], F32)
        NWK = 6
        wk = [sb("wk%d" % i, [128, 528], F32) for i in range(NWK)]
        pl = [sb("pl%d" % i, [128, TT], BF16) for i in range(2)]
        ring = [sb("ring%d" % i, [128, RINGW], BF16) for i in range(NRING)]
        par = sb("par_sb", [128, NPAR], F32)
        sgu = sb("sgu_sb", [128, 1024], F32)
        bsp = sb("bsp", [2, 1024], BF16)
        wsT = sb("wsT_sb", [128, 1024], BF16)
        mask = sb("mask_sb", [128, 128], F32)
        wpool = sb("wpool_sb", [128, 512], BF16)
        onesM = sb("onesM", [128, 128], BF16)
        ones2 = sb("ones2", [2, 128], BF16)
        halo = [sb("halo0", [128, NF, 2], F32), sb("halo1", [128, NF, 2], F32)]
        ss = sb("ss", [128, 8], F32)
        junk = sb("junk", [128, TT], BF16)
        tmp16 = sb("tmp16", [128, 16], F32)
        psum = [es.enter_context(nc.psum_tensor("ps%d" % i, [128, TT], F32)) for i in range(8)]
        big_f = big[:, :].bitcast(F32)
        wsT_f = big_f[:, 0:1024]
        wpool_f = big_f[:, 1024:1536]
        bsp_f = big_f[0:2, 2048:3072]
        bsp_t = big_f[0:2, 3072:4096]
        bsp_hi = big[0:2, 8192:9216]
        BIGALL = [("big", f) for f in range(NF)]

        Sd = Sched()
        for e, ssync, win in [("pe", False, 1), ("act", True, 40), ("dve", True, 40), ("pool", True, 24), ("sp", False, 1)]:
            Sd.add_engine(e, ssync, win)
        PS = Rot(8)
        WK = Rot(NWK)
        PL = Rot(2)
        GV = Rot(2)
        state = {"nload": 0, "deferred": [], "nnorm": 0, "tile": 0}

        def P(name, i=0, n=1):
            o = PAR[name] + i
            return par[:, o:o + n]

        c0 = [
            Sd.op("sp", lambda e: e.dma_start(out=par[:, :], in_=par_d[:, :]), writes=["par"], sem="c0", inc=16, dur=3.0, occ=0.06),
            Sd.op("sp", lambda e: e.dma_start(out=sgu[:, :], in_=sgu_d[:, :]), writes=["sgu"], sem="c0", inc=16, dur=4.0, occ=0.06),
            Sd.op("sp", lambda e: e.dma_start(out=bsp_f, in_=bsp_d[:, :]), writes=["bsp_f"] + BIGALL, sem="c0", inc=16, dur=3.0, occ=0.06),
            Sd.op("sp", lambda e: e.dma_start(out=wsT_f, in_=wsT_d[:, :]), writes=["wsT_f"], sem="c0", inc=16, dur=4.0, occ=0.06),
            Sd.op("sp", lambda e: e.dma_start(out=mask[:, :], in_=mask_d[:, :]), writes=["mask"], sem="c0", inc=16, dur=3.0, occ=0.06),
            Sd.op("sp", lambda e: e.dma_start(out=wpool_f, in_=wpool_d[:, :]), writes=["wpool_f"], sem="c0", inc=16, dur=3.0, occ=0.06),
        ]
        Sd.group_done(c0)

        def x_load(tile):
            hb = tile % NH
            src_ap = xT.rearrange("(c p) t -> p c t", p=128)[:, :, tile * TT:(tile + 1) * TT]
            Sd.op("pool", lambda e: e.dma_start(out=hbuf[hb][:, :, :], in_=src_ap),
                  writes=[("h", hb, c) for c in range(8)], sem="xl%d" % hb, inc=16, dur=14.0, occ=0.5)

        def store(tile):
            hb = tile % NH
            dst = outT.rearrange("(c p) t -> p c t", p=128)[:, :, tile * TT:(tile + 1) * TT]
            Sd.op("pool", lambda e: e.dma_start(out=dst, in_=hbuf[hb][:, :, :]),
                  reads=[("h", hb, c) for c in range(8)], sem="st%d" % hb, inc=16, dur=14.0, occ=0.5)

        x_load(0)
        for t in range(1, min(NH, NT)):
            state["deferred"].append((2 * t, lambda t=t: x_load(t)))
        cast_t = [10.0]

        def cast_group(name, blocks, sem):
            ids = []
            for b in blocks:
                cast_t[0] += 7.0
                ids.append(Sd.op("pool", lambda e, name=name, b=b: e.dma_start(
                    out=scr[name][b * 128:(b + 1) * 128, :], in_=w_d[name][b * 128:(b + 1) * 128, :],
                    max_dma_last_dim=8192),
                    writes=[("scr", name, b)], sem=sem, inc=16, dur=cast_t[0], occ=0.01))
            Sd.group_done(ids)

        ncast = [0]

        def cast_tensor(name, per=4):
            nb = WSPEC[name][0]
            for b0 in range(0, nb, per):
                cast_group(name, list(range(b0, min(nb, b0 + per))), "cast%d" % ncast[0])
                ncast[0] += 1

        if not DIRECT_FIRST:
            for b in (3, 2, 1, 0):
                cast_group("ine", [b], "cast%d" % ncast[0])
                ncast[0] += 1
            for name in ["oute", "gu0", "dn0", "ino", "outo", "gu1", "dn1"]:
                cast_tensor(name)

        Sd.op("dve", lambda e: e.memset(onesM[:, :], 1.0 / 1024.0), writes=["onesM"])
        Sd.op("dve", lambda e: e.memset(ones2[:, :], 1.0), writes=["ones2"])
        Sd.op("dve", lambda e: e.memset(cv[:, :, :], 0.0), writes=[("cv", j) for j in range(4)])
        Sd.op("dve", lambda e: e.memset(zb[:, :, :], 0.0), writes=[("zb", j) for j in range(4)])
        for l in range(2):
            Sd.op("dve", lambda e, l=l: e.memset(halo[l][:, :, :], 0.0), writes=[("halo", l, f) for f in range(NF)])
        Sd.op("dve", lambda e: e.tensor_copy(out=wpool[:, :], in_=wpool_f), reads=["wpool_f"] + BIGALL, writes=["wpool"])
        for hd in range(8):
            Sd.op("dve", lambda e, hd=hd: e.tensor_tensor(
                out=wsT[:, hd * 128:(hd + 1) * 128], in0=wsT_f[:, hd * 128:(hd + 1) * 128], in1=mask[:, :], op=ALU.mult),
                reads=["wsT_f", "mask"] + BIGALL, writes=["wsT"])
        Sd.op("dve", lambda e: e.tensor_copy(out=bsp_hi, in_=bsp_f), reads=["bsp_f"] + BIGALL, writes=["bsp_hi"])
        Sd.op("dve", lambda e: e.tensor_tensor(out=bsp_t, in0=bsp_f, in1=bsp_hi, op=ALU.subtract),
              reads=["bsp_f", "bsp_hi"] + BIGALL, writes=["bsp_t"])
        Sd.op("dve", lambda e: e.tensor_copy(out=bsp[:, :], in_=bsp_t), reads=["bsp_t"] + BIGALL, writes=["bsp"])
        Sd.op("dve", lambda e: e.tensor_copy(out=bsp[0:1, :], in_=bsp_hi[0:1, :]), reads=["bsp_hi", "bsp"] + BIGALL, writes=["bsp"])

        def load_block(name, b):
            slot = state["nload"] % NRING
            state["nload"] += 1
            wd = WSPEC[name][1]
            if DIRECT_FIRST and state["tile"] == 0:
                Sd.op("pool", lambda e: e.dma_start(out=ring[slot][:, 0:wd], in_=w_d[name][b * 128:(b + 1) * 128, :],
                                                    max_dma_last_dim=8192),
                      writes=[("ring", slot)], sem="ringq%d" % slot, inc=16, dur=9.0, occ=0.6)
                Sd.op("sp", lambda e: e.dma_start(out=scr[name][b * 128:(b + 1) * 128, :], in_=ring[slot][:, 0:wd]),
                      reads=[("ring", slot)], writes=[("scr", name, b)], sem="wb%d" % slot, inc=16, dur=6.0, occ=0.06)
            else:
                Sd.op("sp", lambda e: e.dma_start(out=ring[slot][:, 0:wd], in_=scr[name][b * 128:(b + 1) * 128, :]),
                      reads=[("scr", name, b)], writes=[("ring", slot)], sem="ring%d" % slot, inc=16, dur=8.0, occ=0.06)
            nd = []
            for cnt, f in state["deferred"]:
                if cnt <= 1:
                    f()
                else:
                    nd.append((cnt - 1, f))
            state["deferred"] = nd
            return slot

        def mm_group(ps_i, mms, out_ap=None):
            n = len(mms)
            o = psum[ps_i][:, :] if out_ap is None else out_ap
            for i, (lhsT, rhs, reads) in enumerate(mms):
                Sd.op("pe", lambda e, lhsT=lhsT, rhs=rhs, i=i: e.matmul(o, lhsT, rhs, start=(i == 0), stop=(i == n - 1)),
                      reads=reads, writes=[("ps", ps_i)] if i == 0 else [], inc=1 if i == n - 1 else 0)

        def mm_groups_il(groups):
            n = len(groups[0][1])
            ids = [[] for _ in groups]
            for i in range(n):
                for gi, (ps_i, mms) in enumerate(groups):
                    lhsT, rhs, reads = mms[i]
                    o = psum[ps_i][:, :]
                    ids[gi].append(Sd.op("pe", lambda e, o=o, lhsT=lhsT, rhs=rhs, i=i: e.matmul(o, lhsT, rhs, start=(i == 0), stop=(i == n - 1)),
                                         reads=reads, writes=[("ps", ps_i)] if i == 0 else [], inc=1 if i == n - 1 else 0, manual=True))
            for g in ids:
                for i in g:
                    Sd.ops[i].done = g[-1]

        def rmsnorm(hb, gname, out_inplace=False):
            h = hbuf[hb]
            nb = state["nnorm"] % 2
            state["nnorm"] += 1
            sq = sqs[0]
            rstd = rstds[nb]
            for c in range(8):
                Sd.op("act", lambda e, c=c: e.activation(out=sq[:, c, :], in_=h[:, c, :], func=AF.Square),
                      reads=[("h", hb, c)], writes=[("sq", 0, c)])
            for c in range(4):
                Sd.op("dve", lambda e, c=c: e.tensor_tensor(out=sq[:, c, :], in0=sq[:, c, :], in1=sq[:, c + 4, :], op=ALU.add),
                      reads=[("sq", 0, c), ("sq", 0, c + 4)], writes=[("sq", 0, c)], dur=0.5)
            for c in range(2):
                Sd.op("dve", lambda e, c=c: e.tensor_tensor(out=sq[:, c, :], in0=sq[:, c, :], in1=sq[:, c + 2, :], op=ALU.add),
                      reads=[("sq", 0, c), ("sq", 0, c + 2)], writes=[("sq", 0, c)], dur=0.5)
            p = PS.alloc()
            mm_group(p, [(onesM[:, :], sq[:, c, :], ["onesM", ("sq", 0, c)]) for c in range(2)])
            Sd.op("act", lambda e: e.activation(out=rstd[:, :], in_=psum[p][:, :], func=AF.Ln, bias=P("eps")),
                  reads=[("ps", p), "par"], writes=[("rstd", nb)], table="lnexp")
            Sd.op("act", lambda e: e.activation(out=rstd[:, :], in_=rstd[:, :], func=AF.Exp, scale=-0.5),
                  reads=[("rstd", nb)], writes=[("rstd", nb)], table="lnexp")
            for c in range(8):
                if out_inplace:
                    Sd.op("dve", lambda e, c=c: e.scalar_tensor_tensor(
                        out=h[:, c, :], in0=h[:, c, :], scalar=P(gname, c), in1=rstd[:, :], op0=ALU.mult, op1=ALU.mult),
                        reads=[("h", hb, c), ("rstd", nb), "par"], writes=[("h", hb, c)], lag=300)
                else:
                    Sd.op("dve", lambda e, c=c: e.scalar_tensor_tensor(
                        out=xn[:, c, :], in0=h[:, c, :], scalar=P(gname, c), in1=rstd[:, :], op0=ALU.mult, op1=ALU.mult),
                        reads=[("h", hb, c), ("rstd", nb), "par"], writes=[("xn", c)])

        def resid_add(hb, oc, p):
            h = hbuf[hb]
            Sd.op("dve", lambda e: e.tensor_tensor(out=h[:, oc, :], in0=h[:, oc, :], in1=psum[p][:, :], op=ALU.add),
                  reads=[("h", hb, oc), ("ps", p)], writes=[("h", hb, oc)])

        def out_proj(hb, name, kcs=tuple(range(8))):
            for blk in range(2):
                slot = load_block(name, blk)
                for ol in range(4):
                    oc = blk * 4 + ol
                    p = PS.alloc()
                    mm_group(p, [(ring[slot][:, kc * 512 + ol * 128: kc * 512 + (ol + 1) * 128], mix[:, kc, :],
                                  [("ring", slot), ("mix", kc)]) for kc in kcs])
                    resid_add(hb, oc, p)

        def mixer0_j(hb, tile, j, il=False, pre_tail=None):
            slot = load_block("ine", j)
            pidx = {}
            grps = []
            for q in (3, 0, 1, 2):
                p = PS.alloc()
                pidx[q] = p
                grps.append((p, [(ring[slot][:, kc * 512 + q * 128: kc * 512 + (q + 1) * 128], xn[:, kc, :],
                                  [("ring", slot), ("xn", kc)]) for kc in range(8)]))
            if il:
                mm_groups_il(grps)
                if pre_tail is not None:
                    pre_tail()
            else:
                for gi, (p, mms) in enumerate(grps):
                    mm_group(p, mms)
                    if gi == 1 and pre_tail is not None:
                        pre_tail()
            pc, pv, pb, pz = pidx[0], pidx[1], pidx[2], pidx[3]
            t1 = WK.alloc()
            Sd.op("act", lambda e: e.activation(out=wk[t1][:, 0:TT], in_=psum[pc][:, :], func=AF.Copy),
                  reads=[("ps", pc)], writes=[("wk", t1)])
            Sd.op("dve", lambda e: e.tensor_tensor(out=cv[:, j, 2:TT + 2], in0=wk[t1][:, 0:TT], in1=psum[pv][:, :], op=ALU.mult),
                  reads=[("wk", t1), ("ps", pv)], writes=[("cv", j)])
            y = WK.alloc()
            ca = PAR["conv_a"]
            Sd.op("dve", lambda e: e.tensor_scalar(out=wk[y][:, 0:TT], in0=cv[:, j, 0:TT], scalar1=par[:, ca + j: ca + j + 1],
                                                    scalar2=None, op0=ALU.mult),
                  reads=[("cv", j), "par"], writes=[("wk", y)])
            for k in (1, 2):
                Sd.op("dve", lambda e, k=k: e.scalar_tensor_tensor(
                    out=wk[y][:, 0:TT], in0=cv[:, j, k:TT + k], scalar=par[:, ca + k * 4 + j: ca + k * 4 + j + 1],
                    in1=wk[y][:, 0:TT], op0=ALU.mult, op1=ALU.add),
                    reads=[("cv", j), ("wk", y), "par"], writes=[("wk", y)])
            Sd.op("dve", lambda e: e.tensor_tensor(out=mix[:, j, :], in0=wk[y][:, 0:TT], in1=psum[pb][:, :], op=ALU.mult),
                  reads=[("wk", y), ("ps", pb)], writes=[("mix", j)])
            Sd.op("dve", lambda e: e.tensor_copy(out=cv[:, j, 0:2], in_=cv[:, j, TT:TT + 2]),
                  reads=[("cv", j)], writes=[("cv", j)], dur=0.1)
            Sd.op("act", lambda e: e.activation(out=zb[:, j, 15:TT + 15], in_=psum[pz][:, :], func=AF.Copy),
                  reads=[("ps", pz)], writes=[("zb", j)])
            src = zb[:, j, :]
            skey = ("zb", j)
            sh = 1
            for lvl in range(j + 1):
                d = WK.alloc()
                lo = 2 * sh - 1
                Sd.op("dve", lambda e, src=src, d=d, lo=lo, sh=sh: e.tensor_tensor(
                    out=wk[d][:, lo:TT + 15], in0=src[:, lo:TT + 15], in1=src[:, lo - sh:TT + 15 - sh], op=ALU.add),
                    reads=[skey], writes=[("wk", d)])
                src = wk[d]
                skey = ("wk", d)
                sh *= 2
            w = 2 ** (j + 1)
            pb_i = PL.alloc()
            fsrc = src
            Sd.op("dve", lambda e: e.scalar_tensor_tensor(
                out=pl[pb_i][:, :], in0=fsrc[:, 15:TT + 15], scalar=1.0 / w, in1=zb[:, j, 15:TT + 15],
                op0=ALU.mult, op1=ALU.subtract),
                reads=[skey, ("zb", j)], writes=[("pl", pb_i)])
            if tile == 0:
                rc = PAR["rc"] + j * 16
                Sd.op("dve", lambda e: e.tensor_tensor(out=tmp16[:, :], in0=fsrc[:, 15:31], in1=par[:, rc:rc + 16], op=ALU.mult),
                      reads=[skey, "par"], writes=["tmp16"], dur=0.1)
                Sd.op("dve", lambda e: e.tensor_tensor(out=pl[pb_i][:, 0:16], in0=tmp16[:, :], in1=zb[:, j, 15:31], op=ALU.subtract),
                      reads=["tmp16", ("zb", j), ("pl", pb_i)], writes=[("pl", pb_i)], dur=0.1)
            Sd.op("dve", lambda e: e.tensor_copy(out=zb[:, j, 0:15], in_=zb[:, j, TT:TT + 15]),
                  reads=[("zb", j)], writes=[("zb", j)], dur=0.1)
            def tail():
                pp = PS.alloc()
                mm_group(pp, [(wpool[:, j * 128:(j + 1) * 128], pl[pb_i][:, :], ["wpool", ("pl", pb_i)])])
                psc = PAR["pool_scale"] + j
                Sd.op("act", lambda e: e.activation(out=mix[:, 4 + j, :], in_=psum[pp][:, :], func=AF.Copy, scale=par[:, psc:psc + 1]),
                      reads=[("ps", pp), "par"], writes=[("mix", 4 + j)])
            return tail

        def mixer0(hb, tile, hook):
            tl = None
            for j in (3, 2, 1, 0):
                tl = mixer0_j(hb, tile, j, il=False, pre_tail=tl)
            tl()
            hook()
            out_proj(hb, "oute", kcs=(1, 2, 3, 5, 6, 7, 0, 4))

        def ffn_grps(slot, half):
            pg = PS.alloc()
            gg = (pg, [(ring[slot][:, kc * 256 + half * 128: kc * 256 + (half + 1) * 128], xn[:, kc, :],
                        [("ring", slot), ("xn", kc)]) for kc in range(8)])
            pu = PS.alloc()
            gu = (pu, [(ring[slot][:, (8 + kc) * 256 + half * 128: (8 + kc) * 256 + (half + 1) * 128], xn[:, kc, :],
                        [("ring", slot), ("xn", kc)]) for kc in range(8)])
            return gg, gu

        def ffn_f(l, f, pg, pu):
            cf = PAR["conv_ffn%d" % l]
            bf = PAR["b_ffn%d" % l]
            w_ = WK.alloc()
            g_ = WK.alloc()
            Sd.op("dve", lambda e: e.tensor_copy(out=wk[w_][:, 0:2], in_=halo[l][:, f, :]),
                  reads=[("halo", l, f)], writes=[("wk", w_)], dur=0.1)
            Sd.op("act", lambda e: e.activation(out=wk[w_][:, 2:TT + 2], in_=psum[pg][:, :], func=AF.Copy),
                  reads=[("ps", pg), ("wk", w_)], writes=[("wk", w_)])
            Sd.op("act", lambda e: e.activation(
                out=wk[g_][:, 0:TT], in_=psum[pg][:, :], func=AF.Identity,
                scale=par[:, cf + 2 * NF + f: cf + 2 * NF + f + 1], bias=par[:, bf + f: bf + f + 1]),
                reads=[("ps", pg), "par"], writes=[("wk", g_)])
            for k in (1, 0):
                Sd.op("dve", lambda e, k=k: e.scalar_tensor_tensor(
                    out=wk[g_][:, 0:TT], in0=wk[w_][:, k:TT + k], scalar=par[:, cf + k * NF + f: cf + k * NF + f + 1],
                    in1=wk[g_][:, 0:TT], op0=ALU.mult, op1=ALU.add),
                    reads=[("wk", w_), ("wk", g_), "par"], writes=[("wk", g_)])
            Sd.op("dve", lambda e: e.tensor_copy(out=halo[l][:, f, :], in_=wk[w_][:, TT:TT + 2]),
                  reads=[("wk", w_)], writes=[("halo", l, f)], dur=0.1)
            Sd.op("act", lambda e: e.activation(out=wk[g_][:, 0:TT], in_=wk[g_][:, 0:TT], func=AF.Silu),
                  reads=[("wk", g_)], writes=[("wk", g_)], table="silu")
            Sd.op("dve", lambda e: e.tensor_tensor(
                out=big[:, f * TT:(f + 1) * TT], in0=wk[g_][:, 0:TT], in1=psum[pu][:, :], op=ALU.mult),
                reads=[("wk", g_), ("ps", pu)], writes=[("big", f)])

        def ffn_down(l, hb, oc):
            slot = load_block("dn%d" % l, oc)
            p = PS.alloc()
            mm_group(p, [(ring[slot][:, kc * 128:(kc + 1) * 128], big[:, kc * TT:(kc + 1) * TT],
                          [("ring", slot), ("big", kc)]) for kc in range(NF)])
            resid_add(hb, oc, p)

        def ffn(l, hb, hook):
            for blk in range(11):
                slot = load_block("gu%d" % l, blk)
                for half in range(2):
                    gg, gu = ffn_grps(slot, half)
                    mm_group(*gg)
                    mm_group(*gu)
                    ffn_f(l, blk * 2 + half, gg[0], gu[0])
            hook()
            for oc in range(8):
                ffn_down(l, hb, oc)

        def m1_u(slot, ol, c):
            p = PS.alloc()
            mm_group(p, [(ring[slot][:, kc * 512 + ol * 128: kc * 512 + (ol + 1) * 128], xn[:, kc, :],
                          [("ring", slot), ("xn", kc)]) for kc in range(8)])
            Sd.op("act", lambda e: e.activation(out=big_f[:, c * TT:(c + 1) * TT], in_=psum[p][:, :], func=AF.Gelu),
                  reads=[("ps", p)], writes=[("big", 2 * c), ("big", 2 * c + 1)], table="gelu")

        def m1_v_grp(slot, ts):
            p = PS.alloc()
            return (p, [(xn[:, kc, ts * 128:(ts + 1) * 128], ring[slot][:, kc * 512:(kc + 1) * 512],
                         [("ring", slot), ("xn", kc)]) for kc in range(8)])

        def m1_v_half(p, g, half):
            Sd.op("act", lambda e: e.activation(
                out=gv[g][:, half * 512:(half + 1) * 512], in_=psum[p][:, :], func=AF.Gelu),
                reads=[("ps", p)], writes=[("gv", g, half)], table="gelu")
            Sd.op("act", lambda e: e.activation(
                out=junk[:, :], in_=gv[g][:, half * 512:(half + 1) * 512], func=AF.Square, scale=1.0 / 32.0,
                accum_out=ss[:, half:half + 1]),
                reads=[("gv", g, half)], writes=[("ss", half), "junk"])

        def m1_v(ps2, ts):
            g = GV.alloc()
            for half in range(2):
                m1_v_half(ps2[half], g, half)
            Sd.op("dve", lambda e: e.tensor_scalar(out=ss[:, 2:3], in0=ss[:, 0:1], scalar1=ss[:, 1:2], scalar2=EPS,
                                                    op0=ALU.add, op1=ALU.add),
                  reads=[("ss", 0), ("ss", 1)], writes=[("ss", 2)], dur=0.1)
            Sd.op("act", lambda e: e.activation(out=ss[:, 4:5], in_=ss[:, 2:3], func=AF.Ln),
                  reads=[("ss", 2)], writes=[("ss", 4)], dur=0.25, table="lnexp", urgent=True)
            Sd.op("act", lambda e: e.activation(out=ss[:, 3:4], in_=ss[:, 4:5], func=AF.Exp, scale=-0.5),
                  reads=[("ss", 4)], writes=[("ss", 3)], dur=0.25, table="lnexp", urgent=True)
            for half in range(2):
                Sd.op("dve", lambda e, half=half: e.scalar_tensor_tensor(
                    out=vn[:, ts, half * 512:(half + 1) * 512], in0=gv[g][:, half * 512:(half + 1) * 512],
                    scalar=ss[:, 3:4], in1=sgu[:, half * 512:(half + 1) * 512], op0=ALU.mult, op1=ALU.mult),
                    reads=[("gv", g, half), ("ss", 3), "sgu"], writes=[("vn", ts)])

        def m1_gate(hd):
            p = PS.alloc()
            for ts in range(4):
                o = psum[p][:, ts * 128:(ts + 1) * 128]
                Sd.op("pe", lambda e, o=o, ts=ts: e.matmul(o, vn[:, ts, hd * 128:(hd + 1) * 128], wsT[:, hd * 128:(hd + 1) * 128],
                                                           start=True, stop=False),
                      reads=[("vn", ts), "wsT"], writes=[("ps", p)] if ts == 0 else [], inc=0, dur=0.25, occ=0.08)
                Sd.op("pe", lambda e, o=o: e.matmul(o, ones2[0:2, :], bsp[0:2, hd * 128:(hd + 1) * 128], start=False, stop=True),
                      reads=["ones2", "bsp"], writes=[], inc=1 if ts == 3 else 0, dur=0.25, occ=0.06)
            Sd.op("dve", lambda e: e.tensor_tensor(out=mix[:, hd, :], in0=big_f[:, hd * TT:(hd + 1) * TT], in1=psum[p][:, :], op=ALU.mult),
                  reads=[("big", 2 * hd), ("big", 2 * hd + 1), ("ps", p)], writes=[("mix", hd)])

        def mixer1(hb, hook):
            slots = [load_block("ino", 2), load_block("ino", 3)]
            for ts in range(4):
                g2 = [m1_v_grp(slots[half], ts) for half in (0, 1)]
                for p, mms in g2:
                    mm_group(p, mms)
                m1_v([g2[0][0], g2[1][0]], ts)
            for blk in range(2):
                slot = load_block("ino", blk)
                for ol in range(4):
                    m1_u(slot, ol, blk * 4 + ol)
            for hd in range(8):
                m1_gate(hd)
            hook()
            out_proj(hb, "outo")

        stage_names = ["m0", "f0", "m1", "f1"][:min(nstage, 4)]
        norm_of = {"m0": "norm_mix0", "f0": "norm_ffn0", "m1": "norm_mix1", "f1": "norm_ffn1"}
        bodies = []
        for pr in range(0, NT, 2):
            tiles = [t for t in (pr, pr + 1) if t < NT]
            for s in stage_names:
                for t in tiles:
                    bodies.append((s, t))
        pending_fin = []

        def fin(t):
            if nstage >= 5:
                rmsnorm(t % NH, "final_norm", out_inplace=True)
            store(t)
            if t + NH < NT:
                x_load(t + NH)

        def make_hook(i):
            def hook():
                while pending_fin:
                    fin(pending_fin.pop(0))
                if i + 1 < len(bodies):
                    s2, t2 = bodies[i + 1]
                    rmsnorm(t2 % NH, norm_of[s2])
            return hook

        s0, t0 = bodies[0]
        rmsnorm(t0 % NH, norm_of[s0])
        for i, (s, t) in enumerate(bodies):
            hb = t % NH
            state["tile"] = t
            hk = make_hook(i)
            if s == "m0":
                mixer0(hb, t, hk)
            elif s == "f0":
                ffn(0, hb, hk)
            elif s == "m1":
                mixer1(hb, hk)
            else:
                ffn(1, hb, hk)
            if s == stage_names[-1]:
                pending_fin.append(t)
        while pending_fin:
            fin(pending_fin.pop(0))
        Sd.schedule()
        Sd.build_streams()
        Sd.final_wait("sp", ["st%d" % i for i in range(NH)])
        Sd.check()

        semnames = sorted(Sd.count.keys())
        sems = {n: es.enter_context(nc.semaphore("s_" + n)) for n in semnames}
        with nc.Block() as block:
            def replay(name, eng):
                pend = []
                for ent in Sd.streams[name]:
                    if ent[0] == "wait":
                        pend.append((ent[1], ent[2]))
                    else:
                        _, fn, sn, inc = ent
                        for s, v in (pend[:-1] if FUSE_WAIT else pend):
                            eng.wait_ge(sems[s], v)
                        ins = fn(eng)
                        if pend and FUSE_WAIT:
                            ins._wait_ge(sems[pend[-1][0]], pend[-1][1])
                        pend = []
                        if inc > 0:
                            ins.then_inc(sems[sn], inc)
                for s, v in pend:
                    eng.wait_ge(sems[s], v)

            @block.tensor
            def _(e):
                replay("pe", e)

            @block.scalar
            def _(e):
                replay("act", e)

            @block.vector
            def _(e):
                replay("dve", e)

            @block.gpsimd
            def _(e):
                replay("pool", e)

            @block.sync
            def _(e):
                replay("sp", e)
    return nc


def _fm(v):
    v = np.asarray(v, np.float32)
    return np.ascontiguousarray(v.reshape(-1, 128).T)


def _blk_cols(w, cols_per_blk):
    K, N = w.shape
    nb = N // cols_per_blk
    a = w.reshape(K // 128, 128, nb, cols_per_blk).transpose(2, 1, 0, 3)
    return np.ascontiguousarray(a.reshape(nb * 128, (K // 128) * cols_per_blk))


def prep_shared(inp):
    f = lambda k: np.asarray(inp[k], np.float32)
    par = np.zeros((128, NPAR), np.float32)

    def put(name, arr):
        arr = np.asarray(arr, np.float32).reshape(128, -1)
        par[:, PAR[name]:PAR[name] + arr.shape[1]] = arr
    put("norm_mix0", _fm(f("norm_mix")[0]))
    put("norm_mix1", _fm(f("norm_mix")[1]))
    put("norm_ffn0", _fm(f("norm_ffn")[0]))
    put("norm_ffn1", _fm(f("norm_ffn")[1]))
    put("final_norm", _fm(f("final_norm")))
    ca = f("conv_a")[0]
    put("conv_a", np.concatenate([_fm(ca[k]) for k in range(3)], axis=1))
    put("pool_scale", _fm(f("pool_scale")[0]))
    for l in range(2):
        cfw = f("conv_ffn")[l]
        put("conv_ffn%d" % l, np.concatenate([_fm(cfw[k]) for k in range(3)], axis=1))
        put("b_ffn%d" % l, _fm(f("b_conv_ffn")[l]))
    rc = np.zeros((4, 16), np.float32)
    for j in range(4):
        w = 2 ** (j + 1)
        rc[j] = 1.0 / np.minimum(np.arange(1, 17), w)
    put("rc", np.broadcast_to(rc.reshape(1, 64), (128, 64)))
    put("eps", np.full((128, 1), EPS, np.float32))

    sh = {"par": par}
    sh["sgu_bc"] = np.ascontiguousarray(np.broadcast_to(f("sgu_norm")[0][None, :], (128, 1024)))
    sh["bsp2"] = np.ascontiguousarray(np.broadcast_to(f("b_spatial")[0].reshape(1, 1024), (2, 1024)))
    ws = f("w_spatial")[0]
    sh["wsT"] = np.ascontiguousarray(ws.transpose(2, 0, 1).reshape(128, 1024))
    sh["maskT"] = np.ascontiguousarray(np.triu(np.ones((128, 128), np.float32)))
    sh["wpool"] = np.ascontiguousarray(f("w_pool")[0].transpose(1, 0, 2).reshape(128, 512))

    wie = f("w_in_even")[0]
    cols = []
    for j in range(4):
        for base in (512, 1024, 0, 1536):
            cols.append(np.arange(base + j * 128, base + (j + 1) * 128))
    sh["w_ine"] = _blk_cols(wie[:, np.concatenate(cols)], 512)
    sh["w_oute"] = _blk_cols(f("w_out_even")[0], 512)
    sh["w_ino"] = _blk_cols(f("w_in_odd")[0], 512)
    sh["w_outo"] = _blk_cols(f("w_out_odd")[0], 512)
    for l in range(2):
        g = f("w_ffn_gate")[l].reshape(8, 128, 11, 256)
        u = f("w_ffn_up")[l].reshape(8, 128, 11, 256)
        gu = np.stack([g, u], axis=0).transpose(3, 2, 0, 1, 4)
        sh["w_gu%d" % l] = np.ascontiguousarray(gu.reshape(11 * 128, 4096))
        sh["w_dn%d" % l] = _blk_cols(f("w_ffn_down")[l], 128)
    return sh


_NC_CACHE = {}


def kernel(**inputs):
    x = np.asarray(inputs["x"], np.float32)
    B = x.shape[0]
    sh = prep_shared(inputs)
    if "nc" not in _NC_CACHE:
        _NC_CACHE["nc"] = build_nc(NT=S // TT, nstage=5)
    nc = _NC_CACHE["nc"]
    in_maps = []
    for b in range(B):
        m = dict(sh)
        m["xT"] = np.ascontiguousarray(x[b].T)
        in_maps.append(m)
    res = run_bass_kernel_spmd(nc, in_maps, core_ids=list(range(B)))
    out = np.stack([np.ascontiguousarray(r["outT"].T) for r in res.results], axis=0)
    return out.astype(np.float32)
```
